# Optimizing a Trainium2 kernel written in Bass

```python
import math
import jax, jax.numpy as jnp
from jax import lax
import numpy as np

D_MODEL = 2048
BATCH = 4
SEQ = 4096
DEPTH = 4

N_MIXERS = 2
N_POOL_LAYERS = (DEPTH + 1) // 2
N_MLA_LAYERS = DEPTH // 2
N_SUBLAYERS = 3
N_MOD = 3
D_FF = 5632
POOL_WINDOWS = (2, 4, 8, 16)
N_POOL_GROUPS = 4
POOL_GROUP_DIM = D_MODEL // N_POOL_GROUPS
MLA_HEADS = 16
Q_LORA_RANK = 512
KV_LORA_RANK = 512
QK_NOPE_DIM = 128
QK_ROPE_DIM = 64
V_HEAD_DIM = 128
QK_HEAD_DIM = QK_NOPE_DIM + QK_ROPE_DIM
ROPE_THETA = 10000.0
Q_BLOCK = 128
NORM_EPS = 1e-6
ADA_INIT = 0.1

kernel_name = "hybrid_pool_mla_macaron_adaln"


def _rms_norm(x, g):
    xf = x.astype(jnp.float32)
    y = xf * lax.rsqrt(jnp.mean(xf * xf, axis=-1, keepdims=True) + NORM_EPS)
    return (y * g.astype(jnp.float32)).astype(x.dtype)


def _swiglu(h, w_gate, w_up, w_down):
    return (jax.nn.silu(h @ w_gate) * (h @ w_up)) @ w_down


def _rope(x, cos, sin):
    x1, x2 = jnp.split(x, 2, axis=-1)
    return jnp.concatenate([x1 * cos - x2 * sin, x2 * cos + x1 * sin], axis=-1)


def _sublayer(x, mod_s, g_pre, g_post, fn, weight):
    shift = mod_s[:, 0, None, :]
    scale = mod_s[:, 1, None, :]
    gate = mod_s[:, 2, None, :]
    h = _rms_norm(x, g_pre) * (1.0 + scale) + shift
    y = _rms_norm(fn(h), g_post)
    return x + weight * (1.0 + gate) * y


def _pool_mixer(h, w_group, b_group, ch_scale):
    s = h.shape[1]
    hf = h.astype(jnp.float32)
    t = jnp.arange(s)
    outs = []
    for g, w in enumerate(POOL_WINDOWS):
        hg = hf[..., g * POOL_GROUP_DIM:(g + 1) * POOL_GROUP_DIM]
        cs = jnp.cumsum(hg, axis=1)
        cs_lag = jnp.pad(cs, ((0, 0), (w, 0), (0, 0)))[:, :s]
        count = jnp.minimum(t + 1, w).astype(jnp.float32)
        pooled = (cs - cs_lag) / count[None, :, None] - hg
        outs.append(pooled.astype(h.dtype) @ w_group[g] + b_group[g])
    return jnp.concatenate(outs, axis=-1) * ch_scale


def _mla(h, positions, w_dq, q_norm, w_uq, w_dkv, kv_norm, w_ukv, w_o):
    b, s, _ = h.shape
    inv_freq = ROPE_THETA ** (-jnp.arange(0, QK_ROPE_DIM, 2, dtype=jnp.float32) / QK_ROPE_DIM)
    ang = positions.astype(jnp.float32)[..., None] * inv_freq
    cos = jnp.cos(ang).astype(h.dtype)
    sin = jnp.sin(ang).astype(h.dtype)
    c_q = _rms_norm(h @ w_dq, q_norm)
    q = (c_q @ w_uq).reshape(b, s, MLA_HEADS, QK_HEAD_DIM)
    q_nope = q[..., :QK_NOPE_DIM]
    q_rope = _rope(q[..., QK_NOPE_DIM:], cos[:, :, None, :], sin[:, :, None, :])
    ckv = h @ w_dkv
    c_kv = _rms_norm(ckv[..., :KV_LORA_RANK], kv_norm)
    k_rope = _rope(ckv[..., KV_LORA_RANK:], cos, sin)
    kv = (c_kv @ w_ukv).reshape(b, s, MLA_HEADS, QK_NOPE_DIM + V_HEAD_DIM)
    k_nope = kv[..., :QK_NOPE_DIM]
    v = kv[..., QK_NOPE_DIM:]
    sm_scale = QK_HEAD_DIM ** -0.5
    neg = jnp.finfo(jnp.float32).min
    outs = []
    for i in range(s // Q_BLOCK):
        qs, qe = i * Q_BLOCK, (i + 1) * Q_BLOCK
        sc = jnp.einsum('bqhd,bkhd->bhqk', q_nope[:, qs:qe], k_nope[:, :qe],
                        preferred_element_type=jnp.float32)
        sc = sc + jnp.einsum('bqhr,bkr->bhqk', q_rope[:, qs:qe], k_rope[:, :qe],
                             preferred_element_type=jnp.float32)
        causal = jnp.arange(qs, qe)[:, None] >= jnp.arange(qe)[None, :]
        sc = jnp.where(causal[None, None], sc * sm_scale, neg)
        p = jax.nn.softmax(sc, axis=-1)
        outs.append(jnp.einsum('bhqk,bkhd->bqhd', p.astype(v.dtype), v[:, :qe]))
    o = jnp.concatenate(outs, axis=1).reshape(b, s, MLA_HEADS * V_HEAD_DIM)
    return o @ w_o


def setup_inputs(seed: int = 0) -> dict:
    key = jax.random.key(seed)
    ks = jax.random.split(key, 24)
    f32 = jnp.float32

    def nrm(k, shape, fan_in, mult=1.0):
        return jax.random.normal(k, shape, f32) * (mult * fan_in ** -0.5)

    def gain(k, shape, s=0.05):
        return 1.0 + s * jax.random.normal(k, shape, f32)

    def small(k, shape):
        return 0.01 * jax.random.normal(k, shape, f32)

    n_mod = N_SUBLAYERS * N_MOD * D_MODEL
    return {
        "x": jax.random.normal(ks[0], (BATCH, SEQ, D_MODEL), f32),
        "c": jax.random.normal(ks[1], (BATCH, D_MODEL), f32),
        "positions": jnp.tile(jnp.arange(SEQ, dtype=jnp.int32)[None, :], (BATCH, 1)),
        "ada_w": nrm(ks[2], (DEPTH, D_MODEL, n_mod), D_MODEL, ADA_INIT),
        "ada_b": small(ks[3], (DEPTH, n_mod)),
        "norm_pre": gain(ks[4], (DEPTH, N_SUBLAYERS, D_MODEL)),
        "norm_post": gain(ks[5], (DEPTH, N_SUBLAYERS, D_MODEL)),
        "ffn_w_gate": nrm(ks[6], (DEPTH, 2, D_MODEL, D_FF), D_MODEL),
        "ffn_w_up": nrm(ks[7], (DEPTH, 2, D_MODEL, D_FF), D_MODEL),
        "ffn_w_down": nrm(ks[8], (DEPTH, 2, D_FF, D_MODEL), D_FF),
        "pool_w": nrm(ks[9], (N_POOL_LAYERS, N_POOL_GROUPS, POOL_GROUP_DIM, POOL_GROUP_DIM), POOL_GROUP_DIM),
        "pool_b": small(ks[10], (N_POOL_LAYERS, N_POOL_GROUPS, POOL_GROUP_DIM)),
        "pool_scale": gain(ks[11], (N_POOL_LAYERS, D_MODEL), 0.1),
        "mla_w_dq": nrm(ks[12], (N_MLA_LAYERS, D_MODEL, Q_LORA_RANK), D_MODEL),
        "mla_q_norm": gain(ks[13], (N_MLA_LAYERS, Q_LORA_RANK)),
        "mla_w_uq": nrm(ks[14], (N_MLA_LAYERS, Q_LORA_RANK, MLA_HEADS * QK_HEAD_DIM), Q_LORA_RANK),
        "mla_w_dkv": nrm(ks[15], (N_MLA_LAYERS, D_MODEL, KV_LORA_RANK + QK_ROPE_DIM), D_MODEL),
        "mla_kv_norm": gain(ks[16], (N_MLA_LAYERS, KV_LORA_RANK)),
        "mla_w_ukv": nrm(ks[17], (N_MLA_LAYERS, KV_LORA_RANK, MLA_HEADS * (QK_NOPE_DIM + V_HEAD_DIM)), KV_LORA_RANK),
        "mla_w_o": nrm(ks[18], (N_MLA_LAYERS, MLA_HEADS * V_HEAD_DIM, D_MODEL), MLA_HEADS * V_HEAD_DIM),
    }


def reference(x, c, positions, ada_w, ada_b, norm_pre, norm_post, ffn_w_gate, ffn_w_up,
              ffn_w_down, pool_w, pool_b, pool_scale, mla_w_dq, mla_q_norm, mla_w_uq,
              mla_w_dkv, mla_kv_norm, mla_w_ukv, mla_w_o):
    b = x.shape[0]
    c_act = jax.nn.silu(c)
    for layer in range(DEPTH):
        mod = (c_act @ ada_w[layer] + ada_b[layer]).reshape(b, N_SUBLAYERS, N_MOD, D_MODEL)
        x = _sublayer(x, mod[:, 0], norm_pre[layer, 0], norm_post[layer, 0],
                      lambda h: _swiglu(h, ffn_w_gate[layer, 0], ffn_w_up[layer, 0], ffn_w_down[layer, 0]),
                      0.5)
        j = layer // N_MIXERS
        if layer % N_MIXERS == 0:
            mixer = lambda h: _pool_mixer(h, pool_w[j], pool_b[j], pool_scale[j])
        else:
            mixer = lambda h: _mla(h, positions, mla_w_dq[j], mla_q_norm[j], mla_w_uq[j],
                                   mla_w_dkv[j], mla_kv_norm[j], mla_w_ukv[j], mla_w_o[j])
        x = _sublayer(x, mod[:, 1], norm_pre[layer, 1], norm_post[layer, 1], mixer, 1.0)
        x = _sublayer(x, mod[:, 2], norm_pre[layer, 2], norm_post[layer, 2],
                      lambda h: _swiglu(h, ffn_w_gate[layer, 1], ffn_w_up[layer, 1], ffn_w_down[layer, 1]),
                      0.5)
    return x
```

```python
import contextlib
import numpy as np
import concourse.bass as bass
import concourse.mybir as mybir
from concourse.bass_utils import run_bass_kernel_spmd

F32 = mybir.dt.float32
BF16 = mybir.dt.bfloat16
I32 = mybir.dt.int32
AF = mybir.ActivationFunctionType
ALU = mybir.AluOpType

D = 2048
DC = D // 128
DFF = 5632
JC = DFF // 128
NT = 2048
TH = 1024
TT = 512
EPS = 1e-6
NCORES = 8


class _Op:
    __slots__ = ("eng", "fn", "waits", "signal", "semval", "dma_sem")

    def __init__(self, eng, fn, dma_sem):
        self.eng = eng
        self.fn = fn
        self.waits = {}
        self.signal = False
        self.semval = None
        self.dma_sem = dma_sem


class Sched:
    ENGS = ("pe", "act", "dve", "pool", "sp")

    def __init__(self, nc):
        self.nc = nc
        self.ops = {e: [] for e in self.ENGS}
        self.last_writer = {}
        self.readers = {}
        self.dma_count = {}
        self.pending = {e: [] for e in self.ENGS}

    def _dep(self, op, prod, kind):
        if prod is op:
            return
        if prod.dma_sem is not None:
            key = ("dma", prod.dma_sem)
            val = self.dma_count[prod.dma_sem]
        else:
            if prod.eng == op.eng and op.dma_sem is None:
                if prod.eng == "pe" or kind == "war":
                    return
            key = ("eng", prod.eng)
            prod.signal = True
            val = prod
        op.waits.setdefault(key, []).append(val)

    def op(self, eng, fn, reads=(), writes=(), dma_sem=None):
        o = _Op(eng, fn, dma_sem)
        if self.pending[eng]:
            for p in self.pending[eng]:
                if isinstance(p, tuple):
                    o.waits.setdefault(p[0], []).append(p[1])
                else:
                    self._dep(o, p, "raw")
            self.pending[eng] = []
        for r in reads:
            w = self.last_writer.get(r)
            if w is not None:
                self._dep(o, w, "raw")
        for r in writes:
            w = self.last_writer.get(r)
            if w is not None:
                self._dep(o, w, "waw")
            for rd in self.readers.get(r, ()):
                self._dep(o, rd, "war")
        if dma_sem is not None:
            c = self.dma_count.get(dma_sem, 0) + 16
            self.dma_count[dma_sem] = c
            o.semval = c
        for r in reads:
            self.readers.setdefault(r, []).append(o)
        for r in writes:
            self.last_writer[r] = o
            self.readers[r] = []
        self.ops[eng].append(o)
        return o

    def barrier(self):
        deps = []
        for e in self.ENGS:
            for o in reversed(self.ops[e]):
                if o.dma_sem is None:
                    deps.append(o)
                    break
        for k, c in self.dma_count.items():
            deps.append((("dma", k), c))
        for e in self.ENGS:
            self.pending[e] = list(deps)
        self.last_writer = {}
        self.readers = {}

    def emit(self):
        nc = self.nc
        for e in self.ENGS:
            n = 0
            for o in self.ops[e]:
                if o.dma_sem is None and o.signal:
                    n += 1
                    o.semval = n
        dma_keys = sorted(self.dma_count.keys(), key=str)
        with contextlib.ExitStack() as es:
            sems = {}
            for e in self.ENGS:
                sems[("eng", e)] = es.enter_context(nc.semaphore(f"s_{e}"))
            for k in dma_keys:
                sems[("dma", k)] = es.enter_context(nc.semaphore(f"d_{k}"))
            block = es.enter_context(nc.Block())
            engobj = {"pe": block.tensor, "act": block.scalar, "dve": block.vector,
                      "pool": block.gpsimd, "sp": block.sync}

            def make_body(e):
                def body(eng):
                    known = {}
                    for o in self.ops[e]:
                        for key, lst in o.waits.items():
                            v = 0
                            for p in lst:
                                pv = p if isinstance(p, int) else p.semval
                                if pv > v:
                                    v = pv
                            if known.get(key, 0) >= v:
                                continue
                            eng.wait_ge(sems[key], v)
                            known[key] = v
                        ins = o.fn(eng)
                        if o.dma_sem is not None:
                            ins.then_inc(sems[("dma", o.dma_sem)], 16)
                        elif o.signal:
                            ins.then_inc(sems[("eng", e)], 1)
                    if e == "sp":
                        for k in dma_keys:
                            eng.wait_ge(sems[("dma", k)], self.dma_count[k])
                return body

            for e in self.ENGS:
                if self.ops[e] or e == "sp":
                    engobj[e](make_body(e))


class Arena:
    def __init__(self, nc, es, kib):
        self.words = kib * 256
        self.t = es.enter_context(nc.sbuf_tensor("arena", [128, self.words], F32))
        self.tb = self.t.bitcast(BF16)
        self.off = 0

    def reset(self, off=0):
        self.off = off

    def f32(self, n):
        a = self.t[:, self.off:self.off + n]
        self.off += n
        assert self.off <= self.words, (self.off, self.words)
        return a

    def bf16(self, n):
        assert n % 2 == 0
        a = self.tb[:, 2 * self.off:2 * self.off + n]
        self.off += n // 2
        assert self.off <= self.words, (self.off, self.words)
        return a


class Ctx:
    def __init__(self, nc, es):
        self.nc = nc
        self.es = es
        self.s = Sched(nc)
        self.ps = [es.enter_context(nc.psum_tensor(f"ps{i}", [128, 512], F32)) for i in range(8)]
        self.ones = es.enter_context(nc.sbuf_tensor("ones", [128, 128], BF16))
        self.s.op("dve", lambda e: e.memset(self.ones[:], 1.0), writes=["ones"])


def rms_stats(cx, tag, src_fn, nchunks, ntt, sq_tiles, stat_banks, rstd, inv_n, width=TT, k_parts=128):
    s = cx.s
    i = 0
    for tt in range(ntt):
        bank = stat_banks[tt % len(stat_banks)]
        for c in range(nchunks):
            ap, res = src_fn(c, tt)
            sq = sq_tiles[i % len(sq_tiles)]
            sqr = ("sq", i % len(sq_tiles))
            if i % 2 == 0:
                s.op("act", lambda e, ap=ap, sq=sq: e.activation(out=sq[:k_parts, :width], in_=ap, func=AF.Square),
                     reads=[res], writes=[sqr])
            else:
                s.op("pool", lambda e, ap=ap, sq=sq: e.tensor_tensor(out=sq[:k_parts, :width], in0=ap, in1=ap, op=ALU.mult),
                     reads=[res], writes=[sqr])
            s.op("pe", lambda e, sq=sq, bank=bank, c=c: e.matmul(bank[:, :width], lhsT=cx.ones[:k_parts, :], rhs=sq[:k_parts, :width],
                                                                  start=(c == 0), stop=(c == nchunks - 1)),
                 reads=[sqr, "ones"], writes=[("ps", id(bank))])
            i += 1
        rs = rstd[:, tt * width:(tt + 1) * width]
        s.op("act", lambda e, rs=rs, bank=bank: e.activation(out=rs, in_=bank[:, :width], func=AF.Sqrt, bias=cx.epsb[:, 0:1], scale=inv_n),
             reads=[("ps", id(bank)), "epsb"], writes=[("rstd", tt)])
        s.op("dve", lambda e, rs=rs: e.reciprocal(out=rs, in_=rs), reads=[("rstd", tt)], writes=[("rstd", tt)])


def load_vecs(cx, pv_ap, pvt, weight):
    s = cx.s
    s.op("sp", lambda e: e.dma_start(out=pvt[:, 0:80], in_=pv_ap), writes=["pvt"], dma_sem="pv")
    A = pvt[:, 80:96]
    C = pvt[:, 96:112]
    B = pvt[:, 0:16]
    s.op("dve", lambda e: e.scalar_tensor_tensor(out=A, in0=pvt[:, 16:32], scalar=1.0, in1=pvt[:, 48:64], op0=ALU.add, op1=ALU.mult),
         reads=["pvt"], writes=["pvA"])
    s.op("dve", lambda e: e.scalar_tensor_tensor(out=C, in0=pvt[:, 32:48], scalar=1.0, in1=pvt[:, 64:80], op0=ALU.add, op1=ALU.mult),
         reads=["pvt"], writes=["pvC"])
    s.op("dve", lambda e: e.tensor_scalar(out=C, in0=C, scalar1=float(weight), scalar2=None, op0=ALU.mult),
         reads=["pvC"], writes=["pvC"])
    return A, B, C


JR = 4
NR = JC // JR
NWS = 3


def emit_ffn(cx, ar, xin_d, xout_d, pv_ap, wg_d, wu_d, wd_d, nt=NT, tag="f"):
    s = cx.s
    nc = cx.nc
    nh = nt // TH
    ntt = TH // TT
    ar.reset(cx.arena_base)
    pvt = ar.f32(112)
    A, B, C = load_vecs(cx, pv_ap, pvt, 0.5)
    xin = ar.f32(DC * TH)
    hT = ar.bf16(DC * TH)
    hid = [ar.bf16(JR * TH) for _ in range(2)]
    wgu = [(ar.bf16(2048), ar.bf16(2048)) for _ in range(NWS)]
    wdb = [ar.bf16(JR * 2048) for _ in range(2)]
    sqt = [ar.bf16(TT) for _ in range(4)]
    sgt = [ar.bf16(TT) for _ in range(2)]
    rstd = ar.f32(TH)
    xr = [ar.f32(TH) for _ in range(3)]
    ps = cx.ps
    gate_ps = [ps[0], ps[1]]
    up_ps = [ps[2], ps[3]]
    dn_ps = [ps[4], ps[5]]
    st_ps = [ps[6], ps[7]]

    def X(c, tt):
        return xin[:, c * TH + tt * TT: c * TH + (tt + 1) * TT]

    def H(c, tt):
        return hT[:, c * TH + tt * TT: c * TH + (tt + 1) * TT]

    units = [(hf, j) for hf in range(nh) for j in range(JC)]
    rounds = [(hf, r) for hf in range(nh) for r in range(NR)]

    def load_unit(u):
        hf, j = units[u]
        slot = u % NWS
        g, up = wgu[slot]
        s.op("pool", lambda e: e.dma_start(out=g, in_=wg_d[j]), writes=[("wg", tag, slot)], dma_sem=f"wgu{slot}")
        s.op("pool", lambda e: e.dma_start(out=up, in_=wu_d[j]), writes=[("wu", tag, slot)], dma_sem=f"wgu{slot}")

    def load_round(ri):
        hf, r = rounds[ri]
        b = ri % 2
        for jj in range(JR):
            j = r * JR + jj
            s.op("pool", lambda e, j=j, jj=jj: e.dma_start(out=wdb[b][:, jj * 2048:(jj + 1) * 2048], in_=wd_d[j]),
                 writes=[("wd", tag, b, jj)], dma_sem=f"wd{b}")

    for u in range(min(NWS, len(units))):
        load_unit(u)
    load_round(0)

    u = 0
    ri = 0
    gi = 0
    di = 0
    for hf in range(nh):
        t0 = hf * TH
        for c in range(DC):
            s.op("sp", lambda e, c=c, t0=t0: e.dma_start(out=xin[:, c * TH:(c + 1) * TH], in_=xin_d[c * 128:(c + 1) * 128, t0:t0 + TH]),
                 reads=[("xd", tag, c, hf)], writes=[("xin", c, 0), ("xin", c, 1)], dma_sem="xin")
        rms_stats(cx, tag, lambda c, tt: (X(c, tt), ("xin", c, tt)), DC, ntt, sqt, st_ps, rstd, 1.0 / D)
        for tt in range(ntt):
            for c in range(DC):
                s.op("dve", lambda e, c=c, tt=tt: e.tensor_tensor(out=X(c, tt), in0=X(c, tt), in1=rstd[:, tt * TT:(tt + 1) * TT], op=ALU.mult),
                     reads=[("xin", c, tt), ("rstd", tt)], writes=[("xin", c, tt)])
                s.op("act", lambda e, c=c, tt=tt: e.activation(out=H(c, tt), in_=X(c, tt), func=AF.Identity,
                                                                 bias=B[:, c:c + 1], scale=A[:, c:c + 1]),
                     reads=[("xin", c, tt), "pvA", "pvt"], writes=[("hT", c, tt)])
        for r in range(NR):
            hb = ri % 2
            if ri + 1 < len(rounds):
                load_round(ri + 1)
            for jj in range(JR):
                slot = u % NWS
                g, up = wgu[slot]
                for tt in range(ntt):
                    gp = gate_ps[gi % 2]
                    upp = up_ps[gi % 2]
                    for k in range(DC):
                        s.op("pe", lambda e, gp=gp, g=g, k=k, tt=tt: e.matmul(gp[:], lhsT=g[:, k * 128:(k + 1) * 128], rhs=H(k, tt),
                                                                               start=(k == 0), stop=(k == DC - 1)),
                             reads=[("wg", tag, slot), ("hT", k, tt)], writes=[("ps", id(gp))])
                    for k in range(DC):
                        s.op("pe", lambda e, upp=upp, up=up, k=k, tt=tt: e.matmul(upp[:], lhsT=up[:, k * 128:(k + 1) * 128], rhs=H(k, tt),
                                                                                   start=(k == 0), stop=(k == DC - 1)),
                             reads=[("wu", tag, slot), ("hT", k, tt)], writes=[("ps", id(upp))])
                    sg = sgt[gi % 2]
                    s.op("act", lambda e, sg=sg, gp=gp: e.activation(out=sg, in_=gp[:], func=AF.Silu),
                         reads=[("ps", id(gp))], writes=[("sg", gi % 2)])
                    hd = hid[hb][:, jj * TH + tt * TT: jj * TH + (tt + 1) * TT]
                    s.op("dve", lambda e, hd=hd, sg=sg, upp=upp: e.tensor_tensor(out=hd, in0=sg, in1=upp[:], op=ALU.mult),
                         reads=[("sg", gi % 2), ("ps", id(upp))], writes=[("hid", hb, jj, tt)])
                    gi += 1
                u += 1
                if u + NWS - 1 < len(units):
                    load_unit(u + NWS - 1)
            for c in range(DC):
                for tt in range(ntt):
                    dp = dn_ps[di % 2]
                    for jj in range(JR):
                        s.op("pe", lambda e, dp=dp, jj=jj, c=c, tt=tt, hb=hb: e.matmul(
                            dp[:], lhsT=wdb[hb][:, jj * 2048 + c * 128: jj * 2048 + (c + 1) * 128],
                            rhs=hid[hb][:, jj * TH + tt * TT: jj * TH + (tt + 1) * TT], start=(jj == 0), stop=(jj == JR - 1)),
                            reads=[("wd", tag, hb, jj), ("hid", hb, jj, tt)], writes=[("ps", id(dp))])
                    eng = "dve" if (di % 2 == 0) else "act"
                    if r == 0:
                        if eng == "dve":
                            s.op("dve", lambda e, dp=dp, c=c, tt=tt: e.tensor_copy(out=X(c, tt), in_=dp[:]),
                                 reads=[("ps", id(dp))], writes=[("xin", c, tt)])
                        else:
                            s.op("act", lambda e, dp=dp, c=c, tt=tt: e.activation(out=X(c, tt), in_=dp[:], func=AF.Identity),
                                 reads=[("ps", id(dp))], writes=[("xin", c, tt)])
                    else:
                        s.op("dve", lambda e, dp=dp, c=c, tt=tt: e.tensor_tensor(out=X(c, tt), in0=X(c, tt), in1=dp[:], op=ALU.add),
                             reads=[("ps", id(dp)), ("xin", c, tt)], writes=[("xin", c, tt)])
                    di += 1
            ri += 1
        rms_stats(cx, tag + "p", lambda c, tt: (X(c, tt), ("xin", c, tt)), DC, ntt, sqt, st_ps, rstd, 1.0 / D)
        for c in range(DC):
            xs = xr[c % 3]
            s.op("sp", lambda e, c=c, xs=xs, t0=t0: e.dma_start(out=xs, in_=xin_d[c * 128:(c + 1) * 128, t0:t0 + TH]),
                 reads=[("xd", tag, c, hf)], writes=[("xr", c % 3)], dma_sem=f"xr{c % 3}")
            yc = xin[:, c * TH:(c + 1) * TH]
            s.op("pool", lambda e, yc=yc: e.tensor_tensor(out=yc, in0=yc, in1=rstd, op=ALU.mult),
                 reads=[("xin", c, 0), ("xin", c, 1), ("rstd", 0), ("rstd", 1)],
                 writes=[("xin", c, 0), ("xin", c, 1)])
            s.op("dve", lambda e, yc=yc, xs=xs, c=c: e.scalar_tensor_tensor(out=xs, in0=yc, scalar=C[:, c:c + 1], in1=xs,
                                                                            op0=ALU.mult, op1=ALU.add),
                 reads=[("xin", c, 0), ("xin", c, 1), ("xr", c % 3), "pvC"], writes=[("xr", c % 3)])
            s.op("sp", lambda e, c=c, xs=xs, t0=t0: e.dma_start(out=xout_d[c * 128:(c + 1) * 128, t0:t0 + TH], in_=xs),
                 reads=[("xr", c % 3)], writes=[("xo", tag, c, hf)], dma_sem=f"xo{c % 3}")


def new_ctx(nc, es, arena_kib=196):
    cx = Ctx(nc, es)
    cx.epsb = es.enter_context(nc.sbuf_tensor("epsb", [128, 1], F32))
    cx.s.op("dve", lambda e: e.memset(cx.epsb[:], EPS), writes=["epsb"])
    cx.ar = Arena(nc, es, arena_kib)
    cx.arena_base = 0
    return cx


def build_ffn_prog(nt=NT):
    nc = bass.Bass("TRN2", target_bir_lowering=False)
    xin_d = nc.dram_tensor("xT", [D, nt], F32, kind="ExternalInput").ap()
    pv = nc.dram_tensor("pv", [128, 80], F32, kind="ExternalInput").ap()
    wg = nc.dram_tensor("wg", [JC, 128, 2048], F32, kind="ExternalInput").ap()
    wu = nc.dram_tensor("wu", [JC, 128, 2048], F32, kind="ExternalInput").ap()
    wd = nc.dram_tensor("wd", [JC, 128, 2048], F32, kind="ExternalInput").ap()
    xo = nc.dram_tensor("xo", [D, nt], F32, kind="ExternalOutput").ap()
    with contextlib.ExitStack() as es:
        cx = new_ctx(nc, es)
        emit_ffn(cx, cx.ar, xin_d, xo, pv, wg, wu, wd, nt=nt)
        cx.s.emit()
    return nc


def lay_w_in(w):
    n = w.shape[1]
    return np.ascontiguousarray(w.reshape(DC, 128, n // 128, 128).transpose(2, 1, 0, 3)).reshape(n // 128, 128, DC * 128)


def lay_vec(v):
    return np.ascontiguousarray(v.reshape(DC, 128).T)


def post_residual(cx, Yc, Ytile, ntt, width, rstd, sqt, st_ps, C, xin_d, xout_d, t0, th, xr, tag):
    s = cx.s
    rms_stats(cx, tag, Ytile, DC, ntt, sqt, st_ps, rstd, 1.0 / D, width=width)
    rres = [("rstd", tt) for tt in range(ntt)]
    for c in range(DC):
        xs = xr[c % len(xr)]
        xres = ("xr", c % len(xr))
        s.op("sp", lambda e, c=c, xs=xs: e.dma_start(out=xs, in_=xin_d[c * 128:(c + 1) * 128, t0:t0 + th]),
             reads=[("xd", c, t0)], writes=[xres], dma_sem=f"xr{c % len(xr)}")
        yc, yres = Yc(c)
        s.op("pool", lambda e, yc=yc: e.tensor_tensor(out=yc, in0=yc, in1=rstd[:, :th], op=ALU.mult),
             reads=yres + rres, writes=yres)
        s.op("dve", lambda e, yc=yc, xs=xs, c=c: e.scalar_tensor_tensor(out=xs, in0=yc, scalar=C[:, c:c + 1], in1=xs,
                                                                        op0=ALU.mult, op1=ALU.add),
             reads=yres + [xres, "pvC"], writes=[xres])
        s.op("sp", lambda e, c=c, xs=xs: e.dma_start(out=xout_d[c * 128:(c + 1) * 128, t0:t0 + th], in_=xs),
             reads=[xres], writes=[("xo", c, t0)], dma_sem=f"xo{c % len(xr)}")


PTH = 512
HALO = 16


def emit_pool(cx, ar, xin_d, xh_d, hm_d, xout_d, pv_ap, pw_d, pb_d, psc_d, invc_d, nt=NT, tag="p"):
    s = cx.s
    ar.reset(cx.arena_base)
    pvt = ar.f32(112)
    A, B, C = load_vecs(cx, pv_ap, pvt, 1.0)
    L = PTH + HALO
    xin = ar.f32(DC * L)
    yacc = ar.f32(DC * PTH)
    pT = ar.bf16(DC * PTH)
    ta = ar.f32(L)
    tb = ar.f32(L)
    tc_ = ar.f32(PTH)
    invc = ar.f32(4 * PTH)
    pw = ar.bf16(4 * 2048)
    pbt = ar.f32(16)
    pst = ar.f32(16)
    hmt = ar.f32(1)
    sqt = [ar.bf16(TT) for _ in range(4)]
    rstd = ar.f32(L)
    rstd2 = ar.f32(PTH)
    xr = [ar.f32(PTH) for _ in range(3)]
    ps = cx.ps
    st_ps = [ps[6], ps[7]]
    mm_ps = [ps[0], ps[1], ps[2], ps[3]]
    s.op("sp", lambda e: e.dma_start(out=pbt, in_=pb_d), writes=["pbt"], dma_sem="pv2")
    s.op("sp", lambda e: e.dma_start(out=pst, in_=psc_d), writes=["pst"], dma_sem="pv2")
    s.op("sp", lambda e: e.dma_start(out=hmt, in_=hm_d), writes=["hmt"], dma_sem="pv2")
    for g in range(4):
        s.op("pool", lambda e, g=g: e.dma_start(out=pw[:, g * 2048:(g + 1) * 2048], in_=pw_d[g]), writes=[("pw", g)], dma_sem="pw")

    def X(c, lo, hi):
        return xin[:, c * L + lo: c * L + hi]

    mi = 0
    for hf in range(nt // PTH):
        t0 = hf * PTH
        for g in range(4):
            s.op("sp", lambda e, g=g, t0=t0: e.dma_start(out=invc[:, g * PTH:(g + 1) * PTH], in_=invc_d[:, g * nt + t0: g * nt + t0 + PTH]),
                 writes=[("invc", g)], dma_sem="invc")
        for c in range(DC):
            if hf == 0:
                s.op("sp", lambda e, c=c: e.dma_start(out=X(c, 0, HALO), in_=xh_d[c * 128:(c + 1) * 128, :]),
                     writes=[("xin", c, 0)], dma_sem="xin")
            else:
                s.op("sp", lambda e, c=c, t0=t0: e.dma_start(out=X(c, 0, HALO), in_=xin_d[c * 128:(c + 1) * 128, t0 - HALO:t0]),
                     writes=[("xin", c, 0)], dma_sem="xin")
            s.op("sp", lambda e, c=c, t0=t0: e.dma_start(out=X(c, HALO, L), in_=xin_d[c * 128:(c + 1) * 128, t0:t0 + PTH]),
                 reads=[("xd", c, t0)], writes=[("xin", c, 1)], dma_sem="xin")
        for tt, (lo, hi) in enumerate([(0, HALO), (HALO, L)]):
            wdt = hi - lo
            bank = st_ps[tt]
            for c in range(DC):
                sq = sqt[c % 4]
                sqr = ("sq", c % 4)
                ap = X(c, lo, hi)
                s.op("act", lambda e, ap=ap, sq=sq, wdt=wdt: e.activation(out=sq[:, :wdt], in_=ap, func=AF.Square),
                     reads=[("xin", c, tt)], writes=[sqr])
                s.op("pe", lambda e, sq=sq, bank=bank, c=c, wdt=wdt: e.matmul(bank[:, :wdt], lhsT=cx.ones[:, :], rhs=sq[:, :wdt],
                                                                               start=(c == 0), stop=(c == DC - 1)),
                     reads=[sqr, "ones"], writes=[("ps", id(bank))])
            rs = rstd[:, lo:hi]
            s.op("act", lambda e, rs=rs, bank=bank, wdt=wdt: e.activation(out=rs, in_=bank[:, :wdt], func=AF.Sqrt, bias=cx.epsb[:, 0:1], scale=1.0 / D),
                 reads=[("ps", id(bank)), "epsb"], writes=[("rstd", tt)])
            s.op("dve", lambda e, rs=rs: e.reciprocal(out=rs, in_=rs), reads=[("rstd", tt)], writes=[("rstd", tt)])
        for c in range(DC):
            res = [("xin", c, 0), ("xin", c, 1)]
            s.op("dve", lambda e, c=c: e.tensor_tensor(out=X(c, 0, L), in0=X(c, 0, L), in1=rstd[:, 0:L], op=ALU.mult),
                 reads=res + [("rstd", 0), ("rstd", 1)], writes=res)
            s.op("act", lambda e, c=c: e.activation(out=X(c, 0, L), in_=X(c, 0, L), func=AF.Identity, bias=B[:, c:c + 1], scale=A[:, c:c + 1]),
                 reads=res + ["pvA", "pvt"], writes=res)
            if hf == 0:
                s.op("dve", lambda e, c=c: e.tensor_scalar(out=X(c, 0, HALO), in0=X(c, 0, HALO), scalar1=hmt[:, 0:1], scalar2=None, op0=ALU.mult),
                     reads=res + ["hmt"], writes=res)
            g = c // 4
            w = 2 << g
            hres = res
            s.op("pool", lambda e, c=c: e.tensor_tensor(out=ta[:, 1:L], in0=X(c, 1, L), in1=X(c, 0, L - 1), op=ALU.add),
                 reads=hres, writes=["ta"])
            S, Sres = ta, "ta"
            if w >= 4:
                s.op("dve", lambda e: e.tensor_tensor(out=tb[:, 3:L], in0=ta[:, 3:L], in1=ta[:, 1:L - 2], op=ALU.add),
                     reads=["ta"], writes=["tb"])
                S, Sres = tb, "tb"
            if w >= 8:
                s.op("pool", lambda e: e.tensor_tensor(out=ta[:, 7:L], in0=tb[:, 7:L], in1=tb[:, 3:L - 4], op=ALU.add),
                     reads=["tb"], writes=["ta"])
                S, Sres = ta, "ta"
            if w >= 16:
                s.op("dve", lambda e: e.tensor_tensor(out=tb[:, 15:L], in0=ta[:, 15:L], in1=ta[:, 7:L - 8], op=ALU.add),
                     reads=["ta"], writes=["tb"])
                S, Sres = tb, "tb"
            s.op("dve", lambda e, S=S, g=g: e.tensor_tensor(out=tc_, in0=S[:, HALO:L], in1=invc[:, g * PTH:(g + 1) * PTH], op=ALU.mult),
                 reads=[Sres, ("invc", g)], writes=["tc"])
            s.op("dve", lambda e, c=c: e.tensor_tensor(out=pT[:, c * PTH:(c + 1) * PTH], in0=tc_, in1=X(c, HALO, L), op=ALU.subtract),
                 reads=["tc"] + hres, writes=[("pT", c)])
        for g in range(4):
            for n in range(4):
                c = 4 * g + n
                bank = mm_ps[mi % 4]
                mi += 1
                for k in range(4):
                    s.op("pe", lambda e, bank=bank, g=g, n=n, k=k: e.matmul(
                        bank[:, :PTH], lhsT=pw[:, g * 2048 + k * 512 + n * 128: g * 2048 + k * 512 + (n + 1) * 128],
                        rhs=pT[:, (4 * g + k) * PTH:(4 * g + k + 1) * PTH], start=(k == 0), stop=(k == 3)),
                        reads=[("pw", g), ("pT", 4 * g + k)], writes=[("ps", id(bank))])
                s.op("dve", lambda e, bank=bank, c=c: e.tensor_scalar(out=yacc[:, c * PTH:(c + 1) * PTH], in0=bank[:, :PTH],
                                                                      scalar1=pbt[:, c:c + 1], scalar2=pst[:, c:c + 1], op0=ALU.add, op1=ALU.mult),
                     reads=[("ps", id(bank)), "pbt", "pst"], writes=[("y", c)])
        post_residual(cx, lambda c: (yacc[:, c * PTH:(c + 1) * PTH], [("y", c)]),
                      lambda c, tt: (yacc[:, c * PTH:(c + 1) * PTH], ("y", c)), 1, PTH, rstd2, sqt, st_ps, C,
                      xin_d, xout_d, t0, PTH, xr, tag)


def build_pool_prog(nt=NT):
    nc = bass.Bass("TRN2", target_bir_lowering=False)
    xin_d = nc.dram_tensor("xT", [D, nt], F32, kind="ExternalInput").ap()
    xh = nc.dram_tensor("xh", [D, HALO], F32, kind="ExternalInput").ap()
    hm = nc.dram_tensor("hm", [128, 1], F32, kind="ExternalInput").ap()
    pv = nc.dram_tensor("pv", [128, 80], F32, kind="ExternalInput").ap()
    pw = nc.dram_tensor("pw", [4, 128, 2048], F32, kind="ExternalInput").ap()
    pb = nc.dram_tensor("pb", [128, 16], F32, kind="ExternalInput").ap()
    psc = nc.dram_tensor("psc", [128, 16], F32, kind="ExternalInput").ap()
    invc = nc.dram_tensor("invc", [128, 4 * nt], F32, kind="ExternalInput").ap()
    xo = nc.dram_tensor("xo", [D, nt], F32, kind="ExternalOutput").ap()
    with contextlib.ExitStack() as es:
        cx = new_ctx(nc, es)
        emit_pool(cx, cx.ar, xin_d, xh, hm, xo, pv, pw, pb, psc, invc, nt=nt)
        cx.s.emit()
    return nc


NMOD = 144
ADA_SLAB = 9


def emit_ada(cx, ar, c_d, aw_d, ab_d, out_d, nb=2):
    s = cx.s
    ar.reset(cx.arena_base)
    ct = ar.f32(16 * nb)
    cb = ar.bf16(16 * nb)
    slab = [ar.bf16(DC * ADA_SLAB * 128) for _ in range(2)]
    abt = ar.f32(NMOD)
    mo = ar.f32(NMOD * nb)
    ps = cx.ps
    s.op("sp", lambda e: e.dma_start(out=ct, in_=c_d), writes=["ct"], dma_sem="pv")
    s.op("sp", lambda e: e.dma_start(out=abt, in_=ab_d), writes=["abt"], dma_sem="ab")
    s.op("act", lambda e: e.activation(out=cb, in_=ct, func=AF.Silu), reads=["ct"], writes=["cb"])
    nsl = NMOD // ADA_SLAB
    bank = ps[0]
    for sl in range(nsl):
        sb = slab[sl % 2]
        s.op("pool", lambda e, sl=sl, sb=sb: e.dma_start(out=sb, in_=aw_d[sl]), writes=[("slab", sl % 2)], dma_sem=f"slab{sl % 2}")
        for nn in range(ADA_SLAB):
            n = sl * ADA_SLAB + nn
            for k in range(DC):
                s.op("pe", lambda e, sb=sb, nn=nn, k=k, n=n: e.matmul(
                    bank[:, n * nb:(n + 1) * nb], lhsT=sb[:, k * ADA_SLAB * 128 + nn * 128: k * ADA_SLAB * 128 + (nn + 1) * 128],
                    rhs=cb[:, k * nb:(k + 1) * nb], start=(k == 0), stop=(k == DC - 1)),
                    reads=[("slab", sl % 2), "cb"], writes=[("ps", id(bank))])
    for j in range(nb):
        s.op("dve", lambda e, j=j: e.tensor_tensor(out=mo[:, j * NMOD:(j + 1) * NMOD], in0=bank[:, j:NMOD * nb:nb], in1=abt, op=ALU.add),
             reads=[("ps", id(bank)), "abt"], writes=["mo"])
    s.op("sp", lambda e: e.dma_start(out=out_d, in_=mo), reads=["mo"], dma_sem="mo")


def build_ada_prog(nb=2):
    nc = bass.Bass("TRN2", target_bir_lowering=False)
    c_d = nc.dram_tensor("c", [128, 16 * nb], F32, kind="ExternalInput").ap()
    aw = nc.dram_tensor("aw", [NMOD // ADA_SLAB, 128, DC * ADA_SLAB * 128], F32, kind="ExternalInput").ap()
    ab = nc.dram_tensor("ab", [128, NMOD], F32, kind="ExternalInput").ap()
    mo = nc.dram_tensor("mo", [128, NMOD * nb], F32, kind="ExternalOutput").ap()
    with contextlib.ExitStack() as es:
        cx = new_ctx(nc, es)
        emit_ada(cx, cx.ar, c_d, aw, ab, mo, nb)
        cx.s.emit()
    return nc


def lay_ada_w(w):
    nsl = NMOD // ADA_SLAB
    return np.ascontiguousarray(w.reshape(DC, 128, nsl, ADA_SLAB * 128).transpose(2, 1, 0, 3)).reshape(nsl, 128, DC * ADA_SLAB * 128)


def lay_ada_b(b):
    return np.ascontiguousarray(b.reshape(NMOD, 128).T)


QK_NOPE, QK_ROPE, V_DIM = 128, 64, 128
SM_SCALE = float((QK_NOPE + QK_ROPE) ** -0.5)
TWO_PI = 6.283185307179586
CW1 = 6.28125
CW2 = TWO_PI - CW1
NEG = -30000.0


def emit_mla(cx, ar, xf_d, xout_d, posr_d, fs_d, kbias_d, cmask_d, ident_d, pv_ap, qn_d, kvn_d,
             wdq_d, wdkv_d, wkr_d, wkrs_d, wuq_d, wuqr_d, wuqs_d, wuk_d, wuv_d, wo_d, oT_d, nq, H, tag="m"):
    s = cx.s
    nkt = 2 * nq
    W = 512 if nq >= 512 else nq
    ps = cx.ps
    st_ps = [ps[6], ps[7]]
    ar.reset(cx.arena_base)
    pvt = ar.f32(112)
    A, B, C = load_vecs(cx, pv_ap, pvt, 1.0)
    qn = ar.f32(4)
    kvn = ar.f32(4)
    fs = ar.f32(1)
    ident = ar.f32(128)
    cmask = ar.f32(128)
    s.op("sp", lambda e: e.dma_start(out=qn, in_=qn_d), writes=["qn"], dma_sem="pv2")
    s.op("sp", lambda e: e.dma_start(out=kvn, in_=kvn_d), writes=["kvn"], dma_sem="pv2")
    s.op("sp", lambda e: e.dma_start(out=fs[:64, :], in_=fs_d), writes=["fs"], dma_sem="pv2")
    s.op("sp", lambda e: e.dma_start(out=ident, in_=ident_d), writes=["ident"], dma_sem="pv2")
    s.op("sp", lambda e: e.dma_start(out=cmask, in_=cmask_d), writes=["cmask"], dma_sem="pv2")
    sqt = [ar.bf16(512) for _ in range(4)]
    base_o = ar.off
    ckvT = ar.bf16(4 * nkt)
    krT = ar.bf16(nkt)
    cqT = ar.bf16(4 * nq)
    cosq = ar.f32(nq)
    sinq = ar.f32(nq)
    base2 = ar.off
    xt = ar.f32(DC * W)
    hT = ar.bf16(DC * W)
    wdq = ar.bf16(DC * 512)
    wdkv = ar.bf16(DC * 512)
    wkr = ar.bf16(DC * 64)
    wkrs = ar.bf16(DC * 64)
    raw = ar.f32(4 * W)
    rstd = ar.f32(W)
    rstd2 = ar.f32(W)
    posi = ar.t.bitcast(I32)[:, ar.off:ar.off + W]
    ar.off += W
    ang = ar.f32(W)
    kf = ar.f32(W)
    ki = ar.t.bitcast(I32)[:, ar.off:ar.off + W]
    ar.off += W
    cost = ar.f32(W)
    sint = ar.f32(W)
    t1 = ar.f32(W)
    t2 = ar.f32(W)
    s.op("pool", lambda e: e.dma_start(out=wdq, in_=wdq_d), writes=["wdq"], dma_sem="wk")
    s.op("pool", lambda e: e.dma_start(out=wdkv, in_=wdkv_d), writes=["wdkv"], dma_sem="wk")
    s.op("pool", lambda e: e.dma_start(out=wkr, in_=wkr_d), writes=["wkr"], dma_sem="wk")
    s.op("pool", lambda e: e.dma_start(out=wkrs, in_=wkrs_d), writes=["wkrs"], dma_sem="wk")
    bi = 0
    cp = 0

    def evac(out_ap, bank_ap, reads, writes):
        nonlocal cp
        if cp % 2 == 0:
            s.op("act", lambda e: e.activation(out=out_ap, in_=bank_ap, func=AF.Identity), reads=reads, writes=writes)
        else:
            s.op("dve", lambda e: e.tensor_copy(out=out_ap, in_=bank_ap), reads=reads, writes=writes)
        cp += 1

    for ti in range(nkt // W):
        t0 = ti * W
        own = t0 >= nkt - nq
        for c in range(DC):
            s.op("sp", lambda e, c=c, t0=t0: e.dma_start(out=xt[:, c * W:(c + 1) * W], in_=xf_d[c * 128:(c + 1) * 128, t0:t0 + W]),
                 writes=[("xt", c)], dma_sem="xin")
        s.op("sp", lambda e, t0=t0: e.dma_start(out=posi[:64, :], in_=posr_d[:, t0:t0 + W]), writes=["posi"], dma_sem="pos")
        rms_stats(cx, tag, lambda c, tt: (xt[:, c * W:(c + 1) * W], ("xt", c)), DC, 1, sqt, st_ps, rstd, 1.0 / D, width=W)
        for c in range(DC):
            s.op("dve", lambda e, c=c: e.tensor_tensor(out=xt[:, c * W:(c + 1) * W], in0=xt[:, c * W:(c + 1) * W], in1=rstd, op=ALU.mult),
                 reads=[("xt", c), ("rstd", 0)], writes=[("xt", c)])
            s.op("act", lambda e, c=c: e.activation(out=hT[:, c * W:(c + 1) * W], in_=xt[:, c * W:(c + 1) * W], func=AF.Identity,
                                                    bias=B[:, c:c + 1], scale=A[:, c:c + 1]),
                 reads=[("xt", c), "pvA", "pvt"], writes=[("hT", c)])
        s.op("dve", lambda e: e.tensor_copy(out=ang[:64, :], in_=posi[:64, :]), reads=["posi"], writes=["ang"])
        s.op("dve", lambda e: e.tensor_scalar(out=ang[:64, :], in0=ang[:64, :], scalar1=fs[:64, 0:1], scalar2=None, op0=ALU.mult),
             reads=["ang", "fs"], writes=["ang"])
        s.op("dve", lambda e: e.tensor_scalar(out=kf[:64, :], in0=ang[:64, :], scalar1=1.0 / TWO_PI, scalar2=None, op0=ALU.mult),
             reads=["ang"], writes=["kf"])
        s.op("dve", lambda e: e.tensor_copy(out=ki[:64, :], in_=kf[:64, :]), reads=["kf"], writes=["ki"])
        s.op("dve", lambda e: e.tensor_copy(out=kf[:64, :], in_=ki[:64, :]), reads=["ki"], writes=["kf"])
        s.op("dve", lambda e: e.scalar_tensor_tensor(out=ang[:64, :], in0=kf[:64, :], scalar=-CW1, in1=ang[:64, :], op0=ALU.mult, op1=ALU.add),
             reads=["kf", "ang"], writes=["ang"])
        s.op("dve", lambda e: e.scalar_tensor_tensor(out=ang[:64, :], in0=kf[:64, :], scalar=-CW2, in1=ang[:64, :], op0=ALU.mult, op1=ALU.add),
             reads=["kf", "ang"], writes=["ang"])
        PI = float(np.pi)
        for dst, dres, shift in ((sint, "sint", 0.0), (cost, "cost", PI / 2)):
            s.op("dve", lambda e, dst=dst, shift=shift: e.tensor_scalar(out=dst[:64, :], in0=ang[:64, :], scalar1=shift, scalar2=None, op0=ALU.add),
                 reads=["ang"], writes=[dres])
            s.op("dve", lambda e, dst=dst: e.tensor_scalar(out=t1[:64, :], in0=dst[:64, :], scalar1=PI, scalar2=-TWO_PI, op0=ALU.is_gt, op1=ALU.mult),
                 reads=[dres], writes=["t1"])
            s.op("dve", lambda e, dst=dst: e.tensor_tensor(out=dst[:64, :], in0=dst[:64, :], in1=t1[:64, :], op=ALU.add),
                 reads=[dres, "t1"], writes=[dres])
            s.op("dve", lambda e, dst=dst: e.tensor_scalar(out=t1[:64, :], in0=dst[:64, :], scalar1=-PI, scalar2=TWO_PI, op0=ALU.is_lt, op1=ALU.mult),
                 reads=[dres], writes=["t1"])
            s.op("dve", lambda e, dst=dst: e.tensor_tensor(out=dst[:64, :], in0=dst[:64, :], in1=t1[:64, :], op=ALU.add),
                 reads=[dres, "t1"], writes=[dres])
            s.op("dve", lambda e, dst=dst: e.tensor_scalar(out=dst[:64, :], in0=dst[:64, :], scalar1=PI, scalar2=-PI, op0=ALU.min, op1=ALU.max),
                 reads=[dres], writes=[dres])
        s.op("act", lambda e: e.activation(out=sint[:64, :], in_=sint[:64, :], func=AF.Sin), reads=["sint"], writes=["sint"])
        s.op("act", lambda e: e.activation(out=cost[:64, :], in_=cost[:64, :], func=AF.Sin), reads=["cost"], writes=["cost"])
        if own:
            q0 = t0 - (nkt - nq)
            s.op("pool", lambda e, q0=q0: e.tensor_copy(out=cosq[:64, q0:q0 + W], in_=cost[:64, :]), reads=["cost"], writes=[("cosq", q0)])
            s.op("pool", lambda e, q0=q0: e.tensor_copy(out=sinq[:64, q0:q0 + W], in_=sint[:64, :]), reads=["sint"], writes=[("sinq", q0)])
        for m in range(4):
            bank = ps[bi % 6]
            bi += 1
            for k in range(DC):
                s.op("pe", lambda e, bank=bank, m=m, k=k: e.matmul(bank[:, :W], lhsT=wdkv[:, k * 512 + m * 128: k * 512 + (m + 1) * 128],
                                                                   rhs=hT[:, k * W:(k + 1) * W], start=(k == 0), stop=(k == DC - 1)),
                     reads=["wdkv", ("hT", k)], writes=[("ps", id(bank))])
            evac(raw[:, m * W:(m + 1) * W], bank[:, :W], [("ps", id(bank))], [("raw", m)])
        rms_stats(cx, tag, lambda c, tt: (raw[:, c * W:(c + 1) * W], ("raw", c)), 4, 1, sqt, st_ps, rstd2, 1.0 / 512, width=W)
        for m in range(4):
            s.op("dve", lambda e, m=m, t0=t0: e.scalar_tensor_tensor(out=ckvT[:, m * nkt + t0: m * nkt + t0 + W], in0=raw[:, m * W:(m + 1) * W],
                                                                     scalar=kvn[:, m:m + 1], in1=rstd2, op0=ALU.mult, op1=ALU.mult),
                 reads=[("raw", m), ("rstd", 0), "kvn"], writes=[("ckvT", ti)])
        b1 = ps[bi % 6]
        b2 = ps[(bi + 1) % 6]
        bi += 2
        for k in range(DC):
            s.op("pe", lambda e, k=k, b1=b1: e.matmul(b1[:64, :W], lhsT=wkr[:, k * 64:(k + 1) * 64], rhs=hT[:, k * W:(k + 1) * W],
                                                      start=(k == 0), stop=(k == DC - 1)),
                 reads=["wkr", ("hT", k)], writes=[("ps", id(b1))])
        for k in range(DC):
            s.op("pe", lambda e, k=k, b2=b2: e.matmul(b2[:64, :W], lhsT=wkrs[:, k * 64:(k + 1) * 64], rhs=hT[:, k * W:(k + 1) * W],
                                                      start=(k == 0), stop=(k == DC - 1)),
                 reads=["wkrs", ("hT", k)], writes=[("ps", id(b2))])
        s.op("dve", lambda e, b1=b1: e.tensor_tensor(out=t1[:64, :], in0=b1[:64, :W], in1=cost[:64, :], op=ALU.mult),
             reads=[("ps", id(b1)), "cost"], writes=["t1"])
        s.op("dve", lambda e, b2=b2: e.tensor_tensor(out=t2[:64, :], in0=b2[:64, :W], in1=sint[:64, :], op=ALU.mult),
             reads=[("ps", id(b2)), "sint"], writes=["t2"])
        s.op("dve", lambda e, t0=t0: e.tensor_tensor(out=krT[:64, t0:t0 + W], in0=t1[:64, :], in1=t2[:64, :], op=ALU.add),
             reads=["t1", "t2"], writes=[("krT", ti)])
        if own:
            q0 = t0 - (nkt - nq)
            for m in range(4):
                bank = ps[bi % 6]
                bi += 1
                for k in range(DC):
                    s.op("pe", lambda e, bank=bank, m=m, k=k: e.matmul(bank[:, :W], lhsT=wdq[:, k * 512 + m * 128: k * 512 + (m + 1) * 128],
                                                                       rhs=hT[:, k * W:(k + 1) * W], start=(k == 0), stop=(k == DC - 1)),
                         reads=["wdq", ("hT", k)], writes=[("ps", id(bank))])
                evac(raw[:, m * W:(m + 1) * W], bank[:, :W], [("ps", id(bank))], [("raw", m)])
            rms_stats(cx, tag, lambda c, tt: (raw[:, c * W:(c + 1) * W], ("raw", c)), 4, 1, sqt, st_ps, rstd2, 1.0 / 512, width=W)
            for m in range(4):
                s.op("dve", lambda e, m=m, q0=q0: e.scalar_tensor_tensor(out=cqT[:, m * nq + q0: m * nq + q0 + W], in0=raw[:, m * W:(m + 1) * W],
                                                                         scalar=qn[:, m:m + 1], in1=rstd2, op0=ALU.mult, op1=ALU.mult),
                     reads=[("raw", m), ("rstd", 0), "qn"], writes=[("cqT", q0)])
    s.barrier()
    ar.reset(base2)
    wuq_b = [ar.bf16(4 * 128) for _ in range(2)]
    wuqr_b = [ar.bf16(4 * 64) for _ in range(2)]
    wuqs_b = [ar.bf16(4 * 64) for _ in range(2)]
    wuk_b = [ar.bf16(4 * 128) for _ in range(2)]
    wuv_b = [ar.bf16(4 * 128) for _ in range(2)]
    kb_t = ar.bf16(nkt)
    KT = ar.bf16(nkt)
    Vt = ar.bf16(nkt)
    qT = ar.bf16(nq)
    qrT = ar.bf16(nq)
    sc = ar.f32(nkt)
    P = ar.bf16(nkt)
    PT = [ar.bf16(512) for _ in range(2)]
    Dg = ar.bf16(128)
    oTh = [ar.bf16(nq) for _ in range(2)]
    mx = ar.f32(2)
    rsum = ar.f32(2)
    u1 = ar.f32(W)
    u2 = ar.f32(W)
    def load_head_w(h):
        b = h % 2
        for nm, t, d in (("wuq", wuq_b, wuq_d), ("wuqr", wuqr_b, wuqr_d), ("wuqs", wuqs_b, wuqs_d), ("wuk", wuk_b, wuk_d), ("wuv", wuv_b, wuv_d)):
            s.op("pool", lambda e, t=t, d=d, h=h, b=b: e.dma_start(out=t[b], in_=d[h]), writes=[(nm, b)], dma_sem=f"wu{b}")
    s.op("pool", lambda e: e.dma_start(out=kb_t, in_=kbias_d), writes=["kbias"], dma_sem="kb")
    load_head_w(0)
    sc_ps = [ps[0], ps[1]]
    tr_ps = [ps[2], ps[3]]
    o_ps = [ps[4], ps[5]]
    pj_ps = [ps[6], ps[7]]
    pj = 0
    si = 0
    tri = 0
    oi = 0
    HS = 128
    HR = 64
    nkb0 = (nkt - nq) // 128
    for h in range(H):
        hb = h % 2
        if h + 1 < H:
            load_head_w(h + 1)
        wuq, wuqr, wuqs, wuk, wuv = wuq_b[hb], wuqr_b[hb], wuqs_b[hb], wuk_b[hb], wuv_b[hb]
        for kc in range(nkt // W):
            bank = pj_ps[pj % 2]
            pj += 1
            for k in range(4):
                s.op("pe", lambda e, bank=bank, k=k, kc=kc, wuk=wuk: e.matmul(bank[:, :W], lhsT=wuk[:, k * HS: (k + 1) * HS],
                                                                          rhs=ckvT[:, k * nkt + kc * W: k * nkt + (kc + 1) * W], start=(k == 0), stop=(k == 3)),
                     reads=[("wuk", hb), ("ckvT", kc)], writes=[("ps", id(bank))])
            evac(KT[:, kc * W:(kc + 1) * W], bank[:, :W], [("ps", id(bank))], [("KT", kc)])
        for kg in range(nkt // 512 if nkt >= 512 else 1):
            bank = pj_ps[pj % 2]
            pj += 1
            nb = min(4, nkt // 128)
            for j in range(nb):
                kb = kg * 4 + j
                for k in range(4):
                    s.op("pe", lambda e, bank=bank, k=k, kb=kb, j=j, wuv=wuv: e.matmul(
                        bank[:, j * 128:(j + 1) * 128], lhsT=ckvT[:, k * nkt + kb * 128: k * nkt + (kb + 1) * 128],
                        rhs=wuv[:, k * HS: (k + 1) * HS], start=(k == 0), stop=(k == 3)),
                        reads=[("wuv", hb), ("ckvT", (kb * 128) // W)], writes=[("ps", id(bank))])
            evac(Vt[:, kg * 512: kg * 512 + nb * 128], bank[:, :nb * 128], [("ps", id(bank))], [("V", kg)])
        for qc in range(nq // W):
            bank = pj_ps[pj % 2]
            pj += 1
            for k in range(4):
                s.op("pe", lambda e, bank=bank, k=k, qc=qc, wuq=wuq: e.matmul(bank[:, :W], lhsT=wuq[:, k * HS: (k + 1) * HS],
                                                                          rhs=cqT[:, k * nq + qc * W: k * nq + (qc + 1) * W], start=(k == 0), stop=(k == 3)),
                     reads=[("wuq", hb), ("cqT", qc * W)], writes=[("ps", id(bank))])
            evac(qT[:, qc * W:(qc + 1) * W], bank[:, :W], [("ps", id(bank))], [("qT", qc)])
            b1 = pj_ps[pj % 2]
            pj += 1
            for k in range(4):
                s.op("pe", lambda e, b1=b1, k=k, qc=qc, wuqr=wuqr: e.matmul(b1[:64, :W], lhsT=wuqr[:, k * HR: (k + 1) * HR],
                                                                      rhs=cqT[:, k * nq + qc * W: k * nq + (qc + 1) * W], start=(k == 0), stop=(k == 3)),
                     reads=[("wuqr", hb), ("cqT", qc * W)], writes=[("ps", id(b1))])
            s.op("dve", lambda e, b1=b1, qc=qc: e.tensor_tensor(out=u1[:64, :], in0=b1[:64, :W], in1=cosq[:64, qc * W:(qc + 1) * W], op=ALU.mult),
                 reads=[("ps", id(b1)), ("cosq", qc * W)], writes=["u1"])
            b2 = pj_ps[pj % 2]
            pj += 1
            for k in range(4):
                s.op("pe", lambda e, b2=b2, k=k, qc=qc, wuqs=wuqs: e.matmul(b2[:64, :W], lhsT=wuqs[:, k * HR: (k + 1) * HR],
                                                                      rhs=cqT[:, k * nq + qc * W: k * nq + (qc + 1) * W], start=(k == 0), stop=(k == 3)),
                     reads=[("wuqs", hb), ("cqT", qc * W)], writes=[("ps", id(b2))])
            s.op("dve", lambda e, b2=b2, qc=qc: e.tensor_tensor(out=u2[:64, :], in0=b2[:64, :W], in1=sinq[:64, qc * W:(qc + 1) * W], op=ALU.mult),
                 reads=[("ps", id(b2)), ("sinq", qc * W)], writes=["u2"])
            s.op("dve", lambda e, qc=qc: e.tensor_tensor(out=qrT[:64, qc * W:(qc + 1) * W], in0=u1[:64, :], in1=u2[:64, :], op=ALU.add),
                 reads=["u1", "u2"], writes=[("qrT", qc)])
        oT = oTh[h % 2]
        for i in range(nq // 128):
            nkb = nkb0 + i + 1
            nk = nkb * 128
            qres = [("qT", (i * 128) // W), ("qrT", (i * 128) // W)]
            for kc in range((nk + 511) // 512):
                wd_ = min(512, nk - kc * 512)
                bank = sc_ps[si % 2]
                si += 1
                kres = [("KT", (kc * 512) // W + j) for j in range(max(1, 512 // W))] if W < 512 else [("KT", kc)]
                s.op("pe", lambda e, bank=bank, i=i, kc=kc, wd_=wd_: e.matmul(bank[:, :wd_], lhsT=qT[:, i * 128:(i + 1) * 128],
                                                                              rhs=KT[:, kc * 512: kc * 512 + wd_], start=True, stop=False),
                     reads=qres + kres, writes=[("ps", id(bank))])
                s.op("pe", lambda e, bank=bank, i=i, kc=kc, wd_=wd_: e.matmul(bank[:, :wd_], lhsT=qrT[:64, i * 128:(i + 1) * 128],
                                                                              rhs=krT[:64, kc * 512: kc * 512 + wd_], start=False, stop=True),
                     reads=qres, writes=[("ps", id(bank))])
                s.op("dve", lambda e, bank=bank, kc=kc, wd_=wd_: e.scalar_tensor_tensor(
                    out=sc[:, kc * 512: kc * 512 + wd_], in0=bank[:, :wd_], scalar=SM_SCALE, in1=kb_t[:, kc * 512: kc * 512 + wd_],
                    op0=ALU.mult, op1=ALU.add), reads=[("ps", id(bank)), "kbias"], writes=["sc"])
            s.op("dve", lambda e, nk=nk: e.tensor_tensor(out=sc[:, nk - 128:nk], in0=sc[:, nk - 128:nk], in1=cmask, op=ALU.add),
                 reads=["sc", "cmask"], writes=["sc"])
            s.op("dve", lambda e, nk=nk: e.tensor_reduce(out=mx[:, 0:1], in_=sc[:, :nk], axis=mybir.AxisListType.X, op=ALU.max),
                 reads=["sc"], writes=["mx"])
            s.op("dve", lambda e: e.tensor_scalar(out=mx[:, 1:2], in0=mx[:, 0:1], scalar1=-1.0, scalar2=None, op0=ALU.mult),
                 reads=["mx"], writes=["nmx"])
            s.op("act", lambda e, nk=nk: e.activation(out=P[:, :nk], in_=sc[:, :nk], func=AF.Exp, bias=mx[:, 1:2], scale=1.0),
                 reads=["sc", "nmx"], writes=["P"])
            s.op("dve", lambda e, nk=nk: e.tensor_reduce(out=rsum[:, 0:1], in_=P[:, :nk], axis=mybir.AxisListType.X, op=ALU.add),
                 reads=["P"], writes=["rsum"])
            s.op("dve", lambda e: e.reciprocal(out=rsum[:, 1:2], in_=rsum[:, 0:1]), reads=["rsum"], writes=["rinv"])
            s.op("dve", lambda e: e.tensor_scalar(out=Dg, in0=ident, scalar1=rsum[:, 1:2], scalar2=None, op0=ALU.mult),
                 reads=["rinv", "ident"], writes=["Dg"])
            ob = o_ps[oi % 2]
            oi += 1
            for kg in range((nkb + 3) // 4):
                nb = min(4, nkb - kg * 4)
                tb_ = tr_ps[tri % 2]
                pt = PT[tri % 2]
                ptr = ("PT", tri % 2)
                tri += 1
                for j in range(nb):
                    kb = kg * 4 + j
                    s.op("pe", lambda e, tb_=tb_, kb=kb, j=j: e.matmul(tb_[:, j * 128:(j + 1) * 128], lhsT=P[:, kb * 128:(kb + 1) * 128], rhs=Dg,
                                                                         start=True, stop=True),
                         reads=["P", "Dg"], writes=[("ps", id(tb_))])
                evac(pt[:, :nb * 128], tb_[:, :nb * 128], [("ps", id(tb_))], [ptr])
                for j in range(nb):
                    kb = kg * 4 + j
                    s.op("pe", lambda e, ob=ob, pt=pt, kb=kb, j=j, nkb=nkb: e.matmul(ob[:, :128], lhsT=Vt[:, kb * 128:(kb + 1) * 128],
                                                                                      rhs=pt[:, j * 128:(j + 1) * 128], start=(kb == 0), stop=(kb == nkb - 1)),
                         reads=[("V", kb // 4), ptr], writes=[("ps", id(ob))])
            evac(oT[:, i * 128:(i + 1) * 128], ob[:, :128], [("ps", id(ob))], [("oT", h % 2)])
        s.op("sp", lambda e, h=h, oT=oT: e.dma_start(out=oT_d[h], in_=oT), reads=[("oT", h % 2)], writes=[("oTd", h)], dma_sem=f"oT{h % 2}")
    s.barrier()
    ar.reset(base_o)
    wo = ar.bf16(H * 2048)
    oTt = ar.bf16(H * W)
    y = ar.f32(DC * W)
    rstd3 = ar.f32(W)
    xr = [ar.f32(W) for _ in range(3)]
    for h in range(H):
        s.op("pool", lambda e, h=h: e.dma_start(out=wo[:, h * 2048:(h + 1) * 2048], in_=wo_d[h]), writes=["wo"], dma_sem="wo")
    xown = xf_d[:, nkt - nq:nkt]
    yi = 0
    for ti in range(nq // W):
        t0 = ti * W
        for h in range(H):
            s.op("sp", lambda e, h=h, t0=t0: e.dma_start(out=oTt[:, h * W:(h + 1) * W], in_=oT_d[h][:, t0:t0 + W]),
                 reads=[("oTd", h)], writes=[("oTt", h)], dma_sem="oTt")
        for c in range(DC):
            bank = ps[yi % 6]
            yi += 1
            for h in range(H):
                s.op("pe", lambda e, bank=bank, h=h, c=c: e.matmul(bank[:, :W], lhsT=wo[:, h * 2048 + c * 128: h * 2048 + (c + 1) * 128],
                                                                   rhs=oTt[:, h * W:(h + 1) * W], start=(h == 0), stop=(h == H - 1)),
                     reads=["wo", ("oTt", h)], writes=[("ps", id(bank))])
            evac(y[:, c * W:(c + 1) * W], bank[:, :W], [("ps", id(bank))], [("y", c)])
        post_residual(cx, lambda c: (y[:, c * W:(c + 1) * W], [("y", c)]), lambda c, tt: (y[:, c * W:(c + 1) * W], ("y", c)),
                      1, W, rstd3, sqt, st_ps, C, xown, xout_d, t0, W, xr, tag)


def build_mla_prog(nq=NT, H=16):
    nc = bass.Bass("TRN2", target_bir_lowering=False)
    nkt = 2 * nq
    di = lambda n, shp, dt=F32: nc.dram_tensor(n, shp, dt, kind="ExternalInput").ap()
    xf = di("xf", [D, nkt])
    posr = di("posr", [64, nkt], I32)
    fs = di("fs", [64, 1])
    kbias = di("kbias", [128, nkt])
    cmask = di("cmask", [128, 128])
    ident = di("ident", [128, 128])
    pv = di("pv", [128, 80])
    qn = di("qn", [128, 4])
    kvn = di("kvn", [128, 4])
    wdq = di("wdq", [128, DC * 512])
    wdkv = di("wdkv", [128, DC * 512])
    wkr = di("wkr", [128, DC * 64])
    wkrs = di("wkrs", [128, DC * 64])
    wuq = di("wuq", [H, 128, 4 * 128])
    wuqr = di("wuqr", [H, 128, 4 * 64])
    wuqs = di("wuqs", [H, 128, 4 * 64])
    wuk = di("wuk", [H, 128, 4 * 128])
    wuv = di("wuv", [H, 128, 4 * 128])
    wo = di("wo", [H, 128, 2048])
    xo = nc.dram_tensor("xo", [D, nq], F32, kind="ExternalOutput").ap()
    oT_d = nc.dram_tensor("oT_scratch", [H, 128, nq], BF16, kind="Internal").ap()
    with contextlib.ExitStack() as es:
        cx = new_ctx(nc, es)
        emit_mla(cx, cx.ar, xf, xo, posr, fs, kbias, cmask, ident, pv, qn, kvn, wdq, wdkv, wkr, wkrs, wuq, wuqr, wuqs, wuk, wuv, wo, oT_d, nq, H)
        cx.s.emit()
    return nc


def lay_rows(w):
    K_, N_ = w.shape[0] // 128, w.shape[1]
    return np.ascontiguousarray(w.reshape(K_, 128, N_).transpose(1, 0, 2)).reshape(128, K_ * N_)


def mla_weight_layout(w_dq, q_norm, w_uq, w_dkv, kv_norm, w_ukv, w_o, H):
    swap = np.concatenate([np.arange(32, 64), np.arange(0, 32)])
    uq = w_uq.reshape(512, H, 192)
    ukv = w_ukv.reshape(512, H, 256)
    kr = w_dkv[:, 512:576]
    return {
        "qn": np.ascontiguousarray(q_norm.reshape(4, 128).T), "kvn": np.ascontiguousarray(kv_norm.reshape(4, 128).T),
        "wdq": lay_rows(w_dq), "wdkv": lay_rows(w_dkv[:, :512]), "wkr": lay_rows(kr), "wkrs": lay_rows(kr[:, swap]),
        "wuq": np.stack([lay_rows(uq[:, h, :128]) for h in range(H)]), "wuqr": np.stack([lay_rows(uq[:, h, 128:]) for h in range(H)]),
        "wuqs": np.stack([lay_rows(uq[:, h, 128:][:, swap]) for h in range(H)]),
        "wuk": np.stack([lay_rows(ukv[:, h, :128]) for h in range(H)]), "wuv": np.stack([lay_rows(ukv[:, h, 128:]) for h in range(H)]),
        "wo": np.ascontiguousarray(w_o.reshape(H, 128, 2048)),
    }


def mla_consts():
    inv_freq = (10000.0 ** (-np.arange(0, 64, 2, dtype=np.float32) / 64)).astype(np.float32)
    fs = np.concatenate([-inv_freq, inv_freq]).reshape(64, 1).astype(np.float32)
    qi = np.arange(128)[:, None]
    ki = np.arange(128)[None, :]
    cmask = np.where(qi >= ki, 0.0, NEG).astype(np.float32)
    return fs, cmask, np.eye(128, dtype=np.float32)


_PROGS = {}


def _prog(name, builder):
    if name not in _PROGS:
        _PROGS[name] = builder()
    return _PROGS[name]


def _run(nc, in_maps):
    res = run_bass_kernel_spmd(nc, in_maps, core_ids=list(range(NCORES)))
    return res.results


def kernel(x, c, positions, ada_w, ada_b, norm_pre, norm_post, ffn_w_gate, ffn_w_up, ffn_w_down, pool_w, pool_b,
           pool_scale, mla_w_dq, mla_q_norm, mla_w_uq, mla_w_dkv, mla_kv_norm, mla_w_ukv, mla_w_o):
    f32 = np.float32
    x = np.asarray(x, f32)
    c = np.asarray(c, f32)
    positions = np.asarray(positions, np.int32)
    B, S, _ = x.shape
    depth = ada_w.shape[0]
    nc_ada = _prog("ada", lambda: build_ada_prog(2))
    maps = []
    for i in range(NCORES):
        l, b0 = i % 4, 2 * (i // 4)
        cc = np.stack([lay_vec(c[b0]), lay_vec(c[b0 + 1])], axis=2).reshape(128, 32)
        maps.append({"c": np.ascontiguousarray(cc), "aw": lay_ada_w(np.asarray(ada_w[l], f32)), "ab": lay_ada_b(np.asarray(ada_b[l], f32))})
    res = _run(nc_ada, maps)
    mod = {}
    for i in range(NCORES):
        l, b0 = i % 4, 2 * (i // 4)
        mo = res[i]["mo"]
        for j in range(2):
            mod[(l, b0 + j)] = mo[:, j * NMOD:(j + 1) * NMOD]
    xT = [np.ascontiguousarray(x[i // 2, (i % 2) * NT:(i % 2 + 1) * NT, :].T) for i in range(NCORES)]

    def pvec(l, sub, b):
        return np.ascontiguousarray(np.concatenate([mod[(l, b)][:, sub * 48:(sub + 1) * 48], lay_vec(np.asarray(norm_pre[l, sub], f32)),
                                                    lay_vec(np.asarray(norm_post[l, sub], f32))], axis=1))

    def run_ffn(l, sub, which):
        nc = _prog("ffn", lambda: build_ffn_prog(NT))
        wg = lay_w_in(np.asarray(ffn_w_gate[l, which], f32))
        wu = lay_w_in(np.asarray(ffn_w_up[l, which], f32))
        wd = np.ascontiguousarray(np.asarray(ffn_w_down[l, which], f32).reshape(JC, 128, D))
        maps = [{"xT": xT[i], "pv": pvec(l, sub, i // 2), "wg": wg, "wu": wu, "wd": wd} for i in range(NCORES)]
        res = _run(nc, maps)
        for i in range(NCORES):
            xT[i] = res[i]["xo"]

    def run_pool(l, j):
        nc = _prog("pool", lambda: build_pool_prog(NT))
        pw = np.ascontiguousarray(np.asarray(pool_w[j], f32).reshape(4, 4, 128, 512).transpose(0, 2, 1, 3)).reshape(4, 128, 2048)
        pb = lay_vec(np.asarray(pool_b[j], f32).reshape(-1))
        psc = lay_vec(np.asarray(pool_scale[j], f32))
        maps = []
        for i in range(NCORES):
            hf = i % 2
            tg = np.arange(NT) + hf * NT
            invc = np.stack([1.0 / np.minimum(tg + 1, w) for w in (2, 4, 8, 16)]).astype(f32).reshape(1, 4 * NT)
            xh = np.ascontiguousarray(xT[i - 1][:, NT - HALO:]) if hf else np.zeros((D, HALO), f32)
            maps.append({"xT": xT[i], "xh": xh, "hm": np.full((128, 1), float(hf), f32), "pv": pvec(l, 1, i // 2), "pw": pw, "pb": pb,
                         "psc": psc, "invc": np.ascontiguousarray(np.broadcast_to(invc, (128, 4 * NT)))})
        res = _run(nc, maps)
        for i in range(NCORES):
            xT[i] = res[i]["xo"]

    def run_mla(l, j):
        H = 16
        nc = _prog("mla", lambda: build_mla_prog(NT, H))
        wl = mla_weight_layout(np.asarray(mla_w_dq[j], f32), np.asarray(mla_q_norm[j], f32), np.asarray(mla_w_uq[j], f32),
                               np.asarray(mla_w_dkv[j], f32), np.asarray(mla_kv_norm[j], f32), np.asarray(mla_w_ukv[j], f32),
                               np.asarray(mla_w_o[j], f32), H)
        fs, cmask, ident = mla_consts()
        maps = []
        for i in range(NCORES):
            b, hf = i // 2, i % 2
            if hf:
                xf = np.concatenate([xT[i - 1], xT[i]], axis=1)
                pos = positions[b]
                kb = np.zeros(2 * NT, f32)
            else:
                xf = np.concatenate([np.zeros((D, NT), f32), xT[i]], axis=1)
                pos = np.concatenate([np.zeros(NT, np.int32), positions[b, :NT]])
                kb = np.concatenate([np.full(NT, NEG, f32), np.zeros(NT, f32)])
            m = {"xf": np.ascontiguousarray(xf), "posr": np.ascontiguousarray(np.broadcast_to(pos[None].astype(np.int32), (64, 2 * NT))),
                 "fs": fs, "kbias": np.ascontiguousarray(np.broadcast_to(kb[None], (128, 2 * NT))), "cmask": cmask, "ident": ident,
                 "pv": pvec(l, 1, b)}
            m.update(wl)
            maps.append(m)
        res = _run(nc, maps)
        for i in range(NCORES):
            xT[i] = res[i]["xo"]

    for l in range(depth):
        run_ffn(l, 0, 0)
        if l % 2 == 0:
            run_pool(l, l // 2)
        else:
            run_mla(l, l // 2)
        run_ffn(l, 2, 1)
    out = np.empty((B, S, D), f32)
    for i in range(NCORES):
        out[i // 2, (i % 2) * NT:(i % 2 + 1) * NT, :] = xT[i].T
    return out
```
